# Optimizing a Trainium2 kernel written in Bass

```python
import jax, jax.numpy as jnp
from jax import lax
import numpy as np

D_MODEL = 2048
BATCH = 2
SEQ = 8192
DEPTH = 1

MIX_WIDTH = D_MODEL
CONV_WIDTH = MIX_WIDTH // 2
RWKV_WIDTH = MIX_WIDTH - CONV_WIDTH
HEAD_DIM = 64
N_RWKV_HEADS = RWKV_WIDTH // HEAD_DIM
N_CONV_GROUPS = CONV_WIDTH // HEAD_DIM
CONV_KERNEL = 31
DECAY_LORA = 96
ICLR_LORA = 96
GATE_LORA = 256
D_FF = ((8 * D_MODEL + 767) // 768) * 256
RMS_EPS = 1e-6
CONV_GN_EPS = 1e-5
RWKV_GN_EPS = 64e-5
N_MOD = 6

CONV_COLS = 2 * CONV_WIDTH
RWKV_COLS = 3 * RWKV_WIDTH + DECAY_LORA + ICLR_LORA + GATE_LORA
IN_COLS = CONV_COLS + RWKV_COLS

kernel_name = "hymba_style_conv_rwkv7_adaln_block"


def rmsnorm(x, g):
    xf = x.astype(jnp.float32)
    y = xf * lax.rsqrt(jnp.mean(xf * xf, axis=-1, keepdims=True) + RMS_EPS)
    return (y * g).astype(x.dtype)


def group_norm(x, n_groups, gain, bias, eps):
    shp = x.shape
    xf = x.astype(jnp.float32).reshape(shp[:-1] + (n_groups, shp[-1] // n_groups))
    mean = jnp.mean(xf, axis=-1, keepdims=True)
    var = jnp.mean(jnp.square(xf - mean), axis=-1, keepdims=True)
    y = ((xf - mean) * lax.rsqrt(var + eps)).reshape(shp)
    return (y * gain + bias).astype(x.dtype)


def token_shift(p, mu):
    prev = jnp.pad(p, ((0, 0), (1, 0), (0, 0)))[:, :-1]
    return p + (prev - p) * mu


def causal_depthwise_conv(u, w, b):
    C = u.shape[-1]
    y = lax.conv_general_dilated(u, w[:, None, :].astype(u.dtype), window_strides=(1,),
                                 padding=[(w.shape[0] - 1, 0)],
                                 dimension_numbers=('NWC', 'WIO', 'NWC'),
                                 feature_group_count=C)
    return y + b


def wkv7_scan(r, decay, k, v, a, b):
    Bn, T, H, N = r.shape
    xs = tuple(jnp.moveaxis(t.astype(jnp.float32), 1, 0) for t in (r, decay, k, v, a, b))

    def step(S, inp):
        r_t, w_t, k_t, v_t, a_t, b_t = inp
        sa = jnp.einsum('bhij,bhj->bhi', S, a_t)
        S = S * w_t[:, :, None, :] + sa[..., None] * b_t[:, :, None, :] + v_t[..., None] * k_t[:, :, None, :]
        y = jnp.einsum('bhij,bhj->bhi', S, r_t)
        return S, y

    S0 = jnp.zeros((Bn, H, N, N), jnp.float32)
    _, ys = lax.scan(step, S0, xs)
    return jnp.moveaxis(ys, 0, 1)


def hybrid_mixer(h, w_in, tshift_mu, conv_w, conv_b, conv_gn_g, conv_gn_b,
                 w0, w_w2, a0, a_w2, g_w2, k_k, k_a, r_k, lnx_g, lnx_b, w_out):
    Bn, T, _ = h.shape
    p = h @ w_in
    p_conv, p_rwkv = p[..., :CONV_COLS], p[..., CONV_COLS:]

    val, gate = p_conv[..., :CONV_WIDTH], p_conv[..., CONV_WIDTH:]
    u = val * jax.nn.sigmoid(gate)
    u = causal_depthwise_conv(u, conv_w, conv_b)
    u = group_norm(u, N_CONV_GROUPS, conv_gn_g, conv_gn_b, CONV_GN_EPS)
    out_conv = jax.nn.silu(u)

    q = token_shift(p_rwkv, tshift_mu)
    o1, o2, o3 = RWKV_WIDTH, 2 * RWKV_WIDTH, 3 * RWKV_WIDTH
    o4, o5 = o3 + DECAY_LORA, o3 + DECAY_LORA + ICLR_LORA
    r, k, v = q[..., :o1], q[..., o1:o2], q[..., o2:o3]
    xw, xa, xg = q[..., o3:o4], q[..., o4:o5], q[..., o5:]

    w_log = -jax.nn.softplus(-(w0 + jnp.tanh(xw) @ w_w2)) - 0.5
    decay = jnp.exp(-jnp.exp(w_log.astype(jnp.float32)))
    a = jax.nn.sigmoid(a0 + xa @ a_w2)
    g = jax.nn.sigmoid(xg) @ g_w2

    hs = (Bn, T, N_RWKV_HEADS, HEAD_DIM)
    kk = (k * k_k).reshape(hs).astype(jnp.float32)
    kk = kk / jnp.maximum(jnp.linalg.norm(kk, axis=-1, keepdims=True), 1e-12)
    k = k * (1.0 + (a - 1.0) * k_a)
    rh, kh, vh = r.reshape(hs), k.reshape(hs), v.reshape(hs)
    ah = a.reshape(hs).astype(jnp.float32)
    y = wkv7_scan(rh, decay.reshape(hs), kh, vh, -kk, kk * ah)
    y = group_norm(y.reshape(Bn, T, RWKV_WIDTH), N_RWKV_HEADS, lnx_g, lnx_b, RWKV_GN_EPS)
    bonus = jnp.sum(rh.astype(jnp.float32) * kh * r_k, axis=-1, keepdims=True) * vh
    out_rwkv = ((y + bonus.reshape(Bn, T, RWKV_WIDTH)) * g).astype(h.dtype)

    return jnp.concatenate([out_conv.astype(h.dtype), out_rwkv], axis=-1) @ w_out


def swiglu(h, w1, w3, w2):
    return (jax.nn.silu(h @ w1) * (h @ w3)) @ w2


def setup_inputs(seed: int = 0) -> dict:
    key = jax.random.key(seed)
    ks = jax.random.split(key, 32)
    L = DEPTH
    f32 = jnp.float32

    def nrm(k, shape, fan_in, scale=1.0):
        return jax.random.normal(k, shape, f32) * (scale * fan_in ** -0.5)

    def small(k, shape, s=0.02):
        return jax.random.normal(k, shape, f32) * s

    return {
        "x": jax.random.normal(ks[0], (BATCH, SEQ, D_MODEL), f32),
        "c": jax.random.normal(ks[1], (BATCH, D_MODEL), f32),
        "w_ada": nrm(ks[2], (L, D_MODEL, N_MOD * D_MODEL), D_MODEL, 0.5),
        "b_ada": small(ks[3], (L, N_MOD * D_MODEL)),
        "norm1_g": 1.0 + small(ks[4], (L, D_MODEL)),
        "w_in": nrm(ks[5], (L, D_MODEL, IN_COLS), D_MODEL),
        "tshift_mu": jax.random.uniform(ks[6], (L, RWKV_COLS), f32),
        "conv_w": nrm(ks[7], (L, CONV_KERNEL, CONV_WIDTH), CONV_KERNEL),
        "conv_b": small(ks[8], (L, CONV_WIDTH)),
        "conv_gn_g": 1.0 + small(ks[9], (L, CONV_WIDTH)),
        "conv_gn_b": small(ks[10], (L, CONV_WIDTH)),
        "w0": jax.random.uniform(ks[11], (L, RWKV_WIDTH), f32, -5.0, 1.0),
        "w_w2": nrm(ks[12], (L, DECAY_LORA, RWKV_WIDTH), DECAY_LORA, 0.5),
        "a0": small(ks[13], (L, RWKV_WIDTH), 0.1),
        "a_w2": nrm(ks[14], (L, ICLR_LORA, RWKV_WIDTH), ICLR_LORA, 0.5),
        "g_w2": nrm(ks[15], (L, GATE_LORA, RWKV_WIDTH), GATE_LORA),
        "k_k": 0.85 + small(ks[16], (L, RWKV_WIDTH), 0.05),
        "k_a": 1.0 + small(ks[17], (L, RWKV_WIDTH), 0.05),
        "r_k": small(ks[18], (L, N_RWKV_HEADS, HEAD_DIM), 0.1),
        "lnx_g": 1.0 + small(ks[19], (L, RWKV_WIDTH)),
        "lnx_b": small(ks[20], (L, RWKV_WIDTH)),
        "w_out": nrm(ks[21], (L, MIX_WIDTH, D_MODEL), MIX_WIDTH),
        "norm2_g": 1.0 + small(ks[22], (L, D_MODEL)),
        "w_ff1": nrm(ks[23], (L, D_MODEL, D_FF), D_MODEL),
        "w_ff3": nrm(ks[24], (L, D_MODEL, D_FF), D_MODEL),
        "w_ff2": nrm(ks[25], (L, D_FF, D_MODEL), D_FF),
        "norm_f_g": 1.0 + small(ks[26], (D_MODEL,)),
    }


def reference(x, c, w_ada, b_ada, norm1_g, w_in, tshift_mu, conv_w, conv_b, conv_gn_g,
              conv_gn_b, w0, w_w2, a0, a_w2, g_w2, k_k, k_a, r_k, lnx_g, lnx_b, w_out,
              norm2_g, w_ff1, w_ff3, w_ff2, norm_f_g):
    sc = jax.nn.silu(c)
    for l in range(DEPTH):
        mod = sc @ w_ada[l] + b_ada[l]
        sh1, sc1, g1, sh2, sc2, g2 = [m[:, None, :] for m in jnp.split(mod, N_MOD, axis=-1)]

        h = rmsnorm(x, norm1_g[l]) * (1.0 + sc1) + sh1
        mix = hybrid_mixer(h, w_in[l], tshift_mu[l], conv_w[l], conv_b[l], conv_gn_g[l],
                           conv_gn_b[l], w0[l], w_w2[l], a0[l], a_w2[l], g_w2[l], k_k[l],
                           k_a[l], r_k[l], lnx_g[l], lnx_b[l], w_out[l])
        x = x + g1 * mix

        h = rmsnorm(x, norm2_g[l]) * (1.0 + sc2) + sh2
        x = x + g2 * swiglu(h, w_ff1[l], w_ff3[l], w_ff2[l])
    return rmsnorm(x, norm_f_g)
```

```python
import contextlib
import numpy as np
import concourse.bass as bass
import concourse.mybir as mybir
from concourse.bass_utils import run_bass_kernel_spmd

F32 = mybir.dt.float32
BF16 = mybir.dt.bfloat16
ALU = mybir.AluOpType
AF = mybir.ActivationFunctionType
AX = mybir.AxisListType

ENGS = ["pe", "act", "dve", "pool", "sp"]
EPOCH = 4000
SELF_WAIT = True

D = 2048
T = 8192
TQ = 2048
DFF = 5632
NFF = DFF // 128
DEC = 0.6065306597126334


class Res:
    __slots__ = ("name", "w", "r", "excl")

    def __init__(self, name="", excl=False):
        self.name = name
        self.w = None
        self.r = []
        self.excl = excl


class Op:
    __slots__ = ("eng", "fn", "deps", "signal", "gidx", "slot", "dval", "inc", "pos")

    def __init__(self, eng, fn):
        self.pos = 0
        self.eng = eng
        self.fn = fn
        self.deps = []
        self.signal = False
        self.gidx = 0
        self.slot = None
        self.dval = 0
        self.inc = 16


class Sched:
    def __init__(self, nc, self_wait=None):
        self_wait = SELF_WAIT if self_wait is None else self_wait
        self.nc = nc
        self.ops = {e: [] for e in ENGS}
        self.slots = {}
        self.self_wait = self_wait
        self.all_ops = []

    def op(self, eng, fn, reads=(), writes=(), slot=None, inc=16, extra_deps=()):
        o = Op(eng, fn)
        deps = []
        seen = set()

        def add(d):
            if d is None or d is o or id(d) in seen:
                return
            seen.add(id(d))
            deps.append(d)

        writes = list(writes) + [r for r in reads if r.excl and r not in writes]
        reads = [r for r in reads if not r.excl]
        for r in reads:
            add(r.w)
        for w in writes:
            add(w.w)
            for rd in w.r:
                add(rd)
        for d in extra_deps:
            add(d)
        o.deps = deps
        for r in reads:
            r.r.append(o)
        for w in writes:
            w.w = o
            w.r = []
        if slot is not None:
            s = self.slots.setdefault(slot, {"count": 0})
            s["count"] += inc
            o.slot = slot
            o.dval = s["count"]
            o.inc = inc
        o.pos = len(self.ops[eng])
        self.ops[eng].append(o)
        self.all_ops.append(o)
        return o

    def barrier(self):
        lasts = []
        for e in ENGS:
            if self.ops[e]:
                lasts.append(self.ops[e][-1])
        slot_last = {}
        for o in self.all_ops:
            if o.slot is not None:
                slot_last[o.slot] = o
        deps = lasts + list(slot_last.values())
        for e in ENGS:
            self.op(e, None, extra_deps=deps)

    def emit(self):
        nc = self.nc
        for o in self.all_ops:
            flat = []

            def expand(d):
                if d.slot is None and d.fn is None:
                    for dd in d.deps:
                        expand(dd)
                else:
                    flat.append(d)
            for d in o.deps:
                expand(d)
            best = {}
            for d in flat:
                key = ("s", d.slot) if d.slot is not None else ("e", d.eng)
                cur = best.get(key)
                if cur is None or (d.dval > cur.dval if d.slot is not None else d.pos > cur.pos):
                    best[key] = d
            o.deps = list(best.values())
        for o in self.all_ops:
            for d in o.deps:
                if d.slot is None:
                    if d.eng == o.eng and (d.eng == "pe" or not self.self_wait):
                        continue
                    d.signal = True
        nsig = {}
        for e in ENGS:
            g = 0
            for o in self.ops[e]:
                if o.fn is None:
                    o.signal = False
                if o.signal and o.slot is None:
                    g += 1
                    o.gidx = g
            nsig[e] = g
        with contextlib.ExitStack() as st:
            esems = {}
            for e in ENGS:
                n_ep = (nsig[e] + EPOCH - 1) // EPOCH
                esems[e] = [st.enter_context(nc.semaphore(f"s_{e}_{k}")) for k in range(n_ep)]
            ssems = {k: st.enter_context(nc.semaphore(f"d_{k}")) for k in self.slots}
            block = st.enter_context(nc.Block())

            def run(ename, eng):
                seen_c = {}
                seen_d = {}

                def wait_for(d):
                    if d.slot is not None:
                        if seen_d.get(d.slot, 0) >= d.dval:
                            return
                        seen_d[d.slot] = d.dval
                        eng.wait_ge(ssems[d.slot], d.dval)
                    else:
                        if d.fn is None:
                            for dd in d.deps:
                                wait_for(dd)
                            return
                        if not d.signal:
                            return
                        if d.eng == ename and (ename == "pe" or not self.self_wait):
                            return
                        if seen_c.get(d.eng, 0) >= d.gidx:
                            return
                        seen_c[d.eng] = d.gidx
                        ep, v = (d.gidx - 1) // EPOCH, (d.gidx - 1) % EPOCH + 1
                        eng.wait_ge(esems[d.eng][ep], v)

                for o in self.ops[ename]:
                    for d in o.deps:
                        wait_for(d)
                    if o.fn is None:
                        continue
                    ins = o.fn(eng)
                    if o.slot is not None:
                        ins.then_inc(ssems[o.slot], o.inc)
                    elif o.signal:
                        ep = (o.gidx - 1) // EPOCH
                        ins.then_inc(esems[ename][ep], 1)

            @block.tensor
            def _(eng):
                run("pe", eng)

            @block.scalar
            def _(eng):
                run("act", eng)

            @block.vector
            def _(eng):
                run("dve", eng)

            @block.gpsimd
            def _(eng):
                run("pool", eng)

            @block.sync
            def _(eng):
                run("sp", eng)
        return nsig


class Tl:
    __slots__ = ("ap", "res")

    def __init__(self, ap, res=None, name=""):
        self.ap = ap
        self.res = res if res is not None else Res(name)

    def __getitem__(self, idx):
        return Tl(self.ap[idx], self.res)

    def re(self, pat, **kw):
        return Tl(self.ap.rearrange(pat, **kw), self.res)

    def sub(self, idx, name=""):
        return Tl(self.ap[idx], Res(name))


class Arena:
    def __init__(self, ap, ncols, name):
        self.ap = ap
        self.ncols = ncols
        self.off = 0
        self.name = name
        self.peak = 0

    def alloc(self, cols, name=""):
        a = self.off
        assert a + cols <= self.ncols, f"arena {self.name} overflow {a}+{cols}>{self.ncols} ({name})"
        self.off += cols
        self.peak = max(self.peak, self.off)
        return Tl(self.ap[:, a:a + cols], Res(name))

    def mark(self):
        return self.off

    def reset(self, m):
        self.off = m


class K:
    def __init__(self, S):
        self.S = S

    @staticmethod
    def _rs(*xs):
        return [x.res for x in xs if isinstance(x, Tl)]

    @staticmethod
    def _a(x):
        return x.ap if isinstance(x, Tl) else x

    def mm(self, out, lhsT, rhs, start, stop=True):
        o, l, r = out.ap, lhsT.ap, rhs.ap
        return self.S.op("pe", lambda e: e.matmul(o, l, r, start=start, stop=stop),
                         reads=[lhsT.res, rhs.res], writes=[out.res])

    def tr(self, out, in_, ident):
        o, i, d = out.ap, in_.ap, ident.ap
        return self.S.op("pe", lambda e: e.transpose(o, i, d), reads=[in_.res, ident.res], writes=[out.res])

    def act(self, out, in_, func, bias=0.0, scale=1.0, accum=None):
        o, i, b, s = out.ap, in_.ap, self._a(bias), self._a(scale)
        ac = accum.ap if accum is not None else None
        wr = [out.res] + ([accum.res] if accum is not None else [])

        def fn(e):
            if ac is not None:
                return e.activation(o, i, func, bias=b, scale=s, accum_out=ac)
            return e.activation(o, i, func, bias=b, scale=s)
        return self.S.op("act", fn, reads=[in_.res] + self._rs(bias, scale), writes=wr)

    def ts(self, eng, out, in0, s1, s2, op0, op1=None):
        o, i, a, b = out.ap, in0.ap, self._a(s1), self._a(s2)

        def fn(e):
            if op1 is None:
                return e.tensor_scalar(o, i, a, None, op0)
            return e.tensor_scalar(o, i, a, b, op0, op1)
        return self.S.op(eng, fn, reads=[in0.res] + self._rs(s1, s2), writes=[out.res])

    def tt(self, eng, out, in0, in1, op):
        o, a, b = out.ap, in0.ap, in1.ap
        return self.S.op(eng, lambda e: e.tensor_tensor(o, a, b, op), reads=[in0.res, in1.res], writes=[out.res])

    def stt(self, eng, out, in0, scalar, in1, op0, op1):
        o, a, s, b = out.ap, in0.ap, self._a(scalar), in1.ap
        return self.S.op(eng, lambda e: e.scalar_tensor_tensor(o, a, s, b, op0, op1),
                         reads=[in0.res, in1.res] + self._rs(scalar), writes=[out.res])

    def cp(self, eng, out, in_):
        o, i = out.ap, in_.ap
        if eng == "act":
            return self.S.op("act", lambda e: e.copy(o, i), reads=[in_.res], writes=[out.res])
        return self.S.op(eng, lambda e: e.tensor_copy(o, i), reads=[in_.res], writes=[out.res])

    def memset(self, eng, out, val):
        o = out.ap
        return self.S.op(eng, lambda e: e.memset(o, val), writes=[out.res])

    def scan(self, out, d0, d1, init, op0, op1):
        o, a, b = out.ap, d0.ap, d1.ap
        return self.S.op("dve", lambda e: e.tensor_tensor_scan(o, a, b, init, op0, op1),
                         reads=[d0.res, d1.res], writes=[out.res])

    def recip(self, out, in_):
        o, i = out.ap, in_.ap
        return self.S.op("dve", lambda e: e.reciprocal(o, i), reads=[in_.res], writes=[out.res])

    def reduce(self, out, in_, op=ALU.add):
        o, i = out.ap, in_.ap
        return self.S.op("dve", lambda e: e.tensor_reduce(o, i, AX.X, op), reads=[in_.res], writes=[out.res])

    def dma(self, q, out, in_, slot, reads=(), writes=(), extra_deps=()):
        o, i = self._a(out), self._a(in_)
        rd = list(reads) + self._rs(in_)
        wr = list(writes) + self._rs(out)
        return self.S.op(q, lambda e: e.dma_start(out=o, in_=i), reads=rd, writes=wr, slot=slot,
                         extra_deps=extra_deps)


def build_program(debug=False, n_super=16, do_phase2=True, stop_after='all', n_st2=4):
    nc = bass.Bass("TRN2", target_bir_lowering=False)

    def din(name, shape):
        return nc.dram_tensor(name, list(shape), F32, kind="ExternalInput").ap()

    xb = din("xb", [T, D])
    xq = din("xq", [TQ + 128, D])
    cT = din("cT", [128, 16])
    w_ada = din("w_ada", [D, 6 * D])
    b_adaT = din("b_adaT", [128, 96])
    n1T = din("n1T", [128, 16])
    n2T = din("n2T", [128, 16])
    nf_bc_d = din("nf_bc", [128, D])
    w_rw = din("w_rw", [D, 1216])
    vecs = din("vecs", [128, 64])
    w_w2c = din("w_w2c", [96, 256])
    a_w2c = din("a_w2c", [96, 256])
    g_w2c = din("g_w2c", [256, 256])
    lnx_bc = din("lnx_bc", [128, 512])
    w_conv_t = din("w_conv_t", [8 * 128, 16 * 256])
    conv_pk = din("conv_pk", [128, 8 * 34])
    w_out_t = din("w_out_t", [8 * 128, 16 * 256])
    w_ff1_t = din("w_ff1_t", [NFF * 128, 16 * 128])
    w_ff3_t = din("w_ff3_t", [NFF * 128, 16 * 128])
    w_ff2 = din("w_ff2", [DFF, D])
    cst_bf = din("cst_bf", [128, 2560])
    cst_f = din("cst_f", [128, 1024])
    out_d = nc.dram_tensor("out", [TQ, D], F32, kind="ExternalOutput").ap()
    dbg = {}
    if debug:
        dbg["rw"] = nc.dram_tensor("dbg_rw", [T, 256], F32, kind="ExternalOutput").ap()
        dbg["mod"] = nc.dram_tensor("dbg_mod", [128, 96], F32, kind="ExternalOutput").ap()
        dbg["x1"] = nc.dram_tensor("dbg_x1", [TQ, D], F32, kind="ExternalOutput").ap()
        dbg["oc"] = nc.dram_tensor("dbg_oc", [128, 16 * 512], F32, kind="ExternalOutput").ap()

    srcq = [nc.dram_tensor(f"srcq{i}", [TQ, 256], BF16) for i in range(4)]
    dstq = [nc.dram_tensor(f"dstq{i}", [4 * TQ, 256], BF16) for i in range(4)]

    NFC = 24300
    NBC = 57000
    with contextlib.ExitStack() as st:
        sbf = st.enter_context(nc.sbuf_tensor("sbf", [128, NFC], F32))
        sbb = st.enter_context(nc.sbuf_tensor("sbb", [128, NBC], BF16))
        ps = st.enter_context(nc.psum_tensor("ps", [128, 6 * 512], F32))
        tp = st.enter_context(nc.psum_tensor("tp", [128, 2048], BF16))
        AFa = Arena(sbf, NFC, "f32")
        ABa = Arena(sbb, NBC, "bf16")
        S = Sched(nc)
        k = K(S)

        def bank(i, name):
            return Tl(ps[:, i * 512:(i + 1) * 512], Res(name, excl=True))

        BIG = [bank(0, "big0"), bank(1, "big1")]
        big_i = [0]

        def nbig():
            b = BIG[big_i[0] % 2]
            big_i[0] += 1
            return b

        SQ = bank(2, "sq")
        BK1 = bank(3, "bk1")
        M1b, M2b = bank(4, "m1"), bank(5, "m2")
        TP = [Tl(tp[:, 0:1024], Res("tp0", excl=True)), Tl(tp[:, 1024:2048], Res("tp1", excl=True))]
        tp_i = [0]

        def ntp():
            b = TP[tp_i[0] % 2]
            tp_i[0] += 1
            return b

        cb = ABa.alloc(2560, "cst_bf")
        k.dma("pool", cb, cst_bf, "c_bf")
        ident = cb[:, 0:128]
        mXY = cb[:, 128:640]
        mBK = cb[:, 640:1152]
        mRB = cb[:, 1152:1408]
        bdm = cb[:, 1408:1536]
        sel = cb[:, 1536:1600]
        hsel = cb[:, 1600:1602]
        cf = AFa.alloc(1024, "cst_f")
        k.dma("sp", cf, cst_f, "c_f")
        scanm = cf[:, 0:512]
        ident_f = cf[:, 512:640]
        ones_f = cf[:, 640:768]
        bo64 = cf[:, 768:896]
        hm = cf[:, 896:898]
        nhm = cf[:, 898:900]
        hflag = cf[:, 900:901]
        qfl = cf[:, 904:908]
        vc = AFa.alloc(64, "vecs")
        k.dma("sp", vc, vecs, "c_v")
        mu = vc[:, 0:10]
        w0v, a0v, kkv, kav, rkv = vc[:, 10:12], vc[:, 12:14], vc[:, 14:16], vc[:, 16:18], vc[:, 18:20]
        cTt = vc[:, 20:36]
        der = AFa.alloc(64, "derived")
        omu = der[:, 0:10]
        oka = der[:, 10:12]
        k.ts("dve", omu, mu, -1.0, 1.0, ALU.mult, ALU.add)
        k.ts("dve", oka, kav, -1.0, 1.0, ALU.mult, ALU.add)
        lnx = AFa.alloc(512, "lnx")
        k.dma("sp", lnx, lnx_bc, "c_l")
        modv = AFa.alloc(96 + 96 + 16 + 16 + 16 + 16, "modv")
        modT = modv[:, 0:96]
        badT = modv[:, 96:192]
        n1t, n2t = modv[:, 192:208], modv[:, 208:224]
        gm1, gm2 = modv[:, 224:240], modv[:, 240:256]
        k.dma("sp", badT, b_adaT, "c_b")
        k.dma("sp", n1t, n1T, "c_n1")
        k.dma("sp", n2t, n2T, "c_n2")
        f_mark0 = AFa.mark()
        b_mark0 = ABa.mark()

        scb = ABa.alloc(16, "scb")
        k.act(scb, cTt, AF.Silu)
        wa = [ABa.alloc(8192, f"wada{i}") for i in range(2)]
        mps = nbig()
        for blk in range(24):
            wt = wa[blk % 2]
            src = w_ada[:, blk * 512:(blk + 1) * 512].rearrange("(k p) c -> p k c", p=128)
            wt3 = wt.re("p (k c) -> p k c", k=16)
            for kq in range(4):
                k.dma("pool", wt3[:, kq * 4:(kq + 1) * 4, :], src[:, kq * 4:(kq + 1) * 4, :], f"wa{blk % 2}")
            for g in range(4):
                m = blk * 4 + g
                for kc in range(16):
                    k.mm(mps[:, m:m + 1], wt[:, kc * 512 + g * 128: kc * 512 + (g + 1) * 128], scb[:, kc:kc + 1],
                         start=(kc == 0), stop=(kc == 15))
        k.tt("dve", modT, mps[:, 0:96], badT, ALU.add)
        sh1, sc1, g1T, sh2, sc2, g2T = [modT[:, i * 16:(i + 1) * 16] for i in range(6)]
        k.stt("dve", gm1, sc1, 1.0, n1t, ALU.add, ALU.mult)
        k.stt("dve", gm2, sc2, 1.0, n2t, ALU.add, ALU.mult)
        def make_gbc(g1bc, g2bc, dg):
            for (gT, gbc) in ((g1T, g1bc), (g2T, g2bc)):
                for kq in range(4):
                    pb = nbig()
                    for j in range(4):
                        kc = kq * 4 + j
                        k.ts("dve", dg, ident_f, gT[:, kc:kc + 1], None, ALU.mult)
                        k.mm(pb[:, j * 128:(j + 1) * 128], ones_f, dg, start=(j == 0), stop=True)
                    k.cp("act", gbc[:, kq * 512:(kq + 1) * 512], pb)
        if debug:
            k.dma("sp", dbg["mod"], modT, "dbg")
        S.barrier()
        AFa.reset(f_mark0)
        ABa.reset(b_mark0)

        def norm_tile(xt, xn, gm, shv, hT, tt, junk, sm):
            ssq, std, rstd = sm[:, 0:1], sm[:, 1:2], sm[:, 2:3]
            k.act(junk, xt, AF.Square, accum=ssq)
            k.act(std, ssq, AF.Sqrt, bias=1e-6, scale=1.0 / D)
            k.recip(rstd, std)
            k.ts("pool", xn, xt, rstd, 1.0, ALU.mult, ALU.mult)

        def transposes(xns, gm, shv, hT, ntt):
            W = 128 * ntt
            for kc in range(16):
                tpb = ntp()
                for tt in range(ntt):
                    k.tr(tpb[:, tt * 128:(tt + 1) * 128], xns[tt][:, kc * 128:(kc + 1) * 128], ident)
                dst = hT[:, kc * 512: kc * 512 + W]
                if kc % 2 == 0:
                    k.act(dst, tpb[:, 0:W], AF.Identity, bias=shv[:, kc:kc + 1], scale=gm[:, kc:kc + 1])
                else:
                    k.ts("dve", dst, tpb[:, 0:W], gm[:, kc:kc + 1], shv[:, kc:kc + 1], ALU.mult, ALU.add)

        wrw = ABa.alloc(16 * 1216, "wrw")
        wrw3 = wrw.re("p (k c) -> p k c", k=16)
        for kq in range(4):
            k.dma("pool", wrw3[:, kq * 4:(kq + 1) * 4, :],
                  w_rw[kq * 512:(kq + 1) * 512, :].rearrange("(k p) c -> p k c", p=128), "c_wrw")
        ww2 = ABa.alloc(256, "ww2")
        aw2 = ABa.alloc(256, "aw2")
        gw2 = ABa.alloc(512, "gw2")
        k.dma("pool", ww2[0:96, :], w_w2c, "c_l1")
        k.dma("pool", aw2[0:96, :], a_w2c, "c_l2")
        k.dma("pool", gw2.re("p (k c) -> p k c", k=2), g_w2c.rearrange("(k p) c -> p k c", p=128), "c_l3")

        xts = [AFa.alloc(2048, f"xt{i}") for i in range(2)]
        xns = [ABa.alloc(2048, f"xn{i}") for i in range(4)]
        sms = [AFa.alloc(4, f"sm{i}") for i in range(2)]
        hT = ABa.alloc(8192, "hT")
        pm = [AFa.alloc(513, f"pm{g}") for g in range(10)]
        for g in range(10):
            k.memset("pool", pm[g][:, 512:513], 0.0)
        rf = [AFa.alloc(512, f"rf{h}") for h in range(2)]
        kf = [AFa.alloc(512, f"kf{h}") for h in range(2)]
        tmpf = AFa.alloc(512, "tmpf")
        swf = AFa.alloc(512, "swf")
        af_ = AFa.alloc(512, "af")
        cum = AFa.alloc(512, "cum")
        cumx = AFa.alloc(512, "cumx")
        Er = [AFa.alloc(512, f"Er{h}") for h in range(2)]
        Ek = AFa.alloc(512, "Ek")
        Ea = AFa.alloc(512, "Ea")
        nrm = AFa.alloc(512, "nrm")
        kkn = AFa.alloc(512, "kkn")
        t1 = AFa.alloc(512, "t1")
        t2 = AFa.alloc(512, "t2")
        kmod = AFa.alloc(512, "kmod")
        gtok = AFa.alloc(1024, "gtok")
        bsf = [AFa.alloc(2, f"bsf{h}") for h in range(2)]
        Hf = [AFa.alloc(64, f"Hf{h}") for h in range(2)]
        Zt = AFa.alloc(64, "Zt")
        ycp = AFa.alloc(128, "ycp")
        yn = AFa.alloc(128, "yn")
        st8 = AFa.alloc(16, "st8")
        vT = [ABa.alloc(512, f"vT{h}") for h in range(2)]
        txw = ABa.alloc(512, "txw")
        xab = ABa.alloc(512, "xab")
        sgb = ABa.alloc(1024, "sgb")
        kk2 = ABa.alloc(512, "kk2")
        ARm = [[ABa.alloc(1024, f"ARm{hp}{h}") for h in range(2)] for hp in range(2)]
        bT = [ABa.alloc(512, f"bT{h}") for h in range(2)]
        kT = [ABa.alloc(512, f"kT{h}") for h in range(2)]
        rkb = [ABa.alloc(512, f"rk{h}") for h in range(2)]
        XY = [ABa.alloc(512, f"XY{l}") for l in range(7)]
        XK = ABa.alloc(512, "XK")
        XB = ABa.alloc(256, "XB")
        Wb = [ABa.alloc(256, f"Wb{i}") for i in range(2)]
        TMbk = ABa.alloc(512, "TMbk")
        k.memset("pool", TMbk, 0.0)
        Vtk = ABa.alloc(128, "Vtk")
        RTm = ABa.alloc(256, "RTm")
        GT = ABa.alloc(128, "GT")
        Hb = [ABa.alloc(64, f"Hb{h}") for h in range(2)]
        otile = [ABa.alloc(256, f"ot{i}") for i in range(2)]
        for h in range(2):
            k.memset("pool", Hf[h], 0.0)
            k.memset("pool", Hb[h], 0.0)

        pMrb = M1b[:, 0:256]
        pW = M1b[:, 256:512]
        pR = M2b[:, 0:256]
        pG = M2b[:, 256:384]
        pH = M2b[:, 384:448]
        pBS = M2b[:, 448:450]
        pY2 = BK1[:, 0:128]

        GW = [128] * 8 + [96, 96]
        GOFF = [0, 128, 256, 384, 512, 640, 768, 896, 1024, 1120]

        cc_ops = []
        for sti in range(n_super):
            t0 = sti * 512
            def s1_tile(si, tt):
                xt = xts[(si * 4 + tt) % 2]
                tb = si * 512
                k.dma("sp", xt, xb[tb + tt * 128: tb + (tt + 1) * 128, :], f"x{(si * 4 + tt) % 2}")
                norm_tile(xt, xns[tt], gm1, sh1, hT, tt, xns[tt], sms[tt % 2])
            if sti == 0:
                for tt in range(4):
                    s1_tile(0, tt)
            transposes(xns, gm1, sh1, hT, 4)
            for g in range(10):
                P = GW[g]
                pb = nbig()
                for kc in range(16):
                    k.mm(pb[0:P, :], wrw[:, kc * 1216 + GOFF[g]: kc * 1216 + GOFF[g] + P],
                         hT[:, kc * 512:(kc + 1) * 512], start=(kc == 0), stop=(kc == 15))
                pmg = pm[g]
                k.cp("act", pmg[0:P, 0:1], pmg[0:P, 512:513])
                k.act(pmg[0:P, 1:513], pb[0:P, :], AF.Identity, scale=mu[0:P, g:g + 1])
                if g < 2:
                    dst = rf[g]
                elif g < 4:
                    dst = kf[g - 2]
                elif g < 6:
                    dst = vT[g - 4]
                elif g < 8:
                    dst = tmpf
                elif g == 8:
                    dst = tmpf
                else:
                    dst = xab
                k.stt("dve", dst[0:P, :], pb[0:P, :], omu[0:P, g:g + 1], pmg[0:P, 0:512], ALU.mult, ALU.add)
                if 6 <= g < 8:
                    k.act(sgb[:, (g - 6) * 512:(g - 5) * 512], tmpf, AF.Sigmoid)
                elif g == 8:
                    k.act(txw[0:96, :], tmpf[0:96, :], AF.Tanh)
            for half in range(2):
                pb = nbig()
                for cc in range(2):
                    c = half * 2 + cc
                    for kc in range(2):
                        k.mm(pb[:, cc * 256:(cc + 1) * 256], sgb[:, kc * 512 + c * 128: kc * 512 + (c + 1) * 128],
                             gw2[:, kc * 256:(kc + 1) * 256], start=(cc == 0 and kc == 0), stop=(kc == 1))
                k.cp("act", gtok[:, half * 512:(half + 1) * 512], pb)
            for hp in range(2):
                pb = nbig()
                k.mm(pb, ww2[0:96, hp * 128:(hp + 1) * 128], txw[0:96, :], start=True)
                k.act(swf, pb, AF.Sigmoid, bias=w0v[:, hp:hp + 1])
                pb = nbig()
                k.mm(pb, aw2[0:96, hp * 128:(hp + 1) * 128], xab[0:96, :], start=True)
                k.act(af_, pb, AF.Sigmoid, bias=a0v[:, hp:hp + 1])
                k.scan(cum, scanm, swf, 0.0, ALU.mult, ALU.add)
                k.tt("pool", cumx, cum, swf, ALU.subtract)
                k.act(Er[hp], cum, AF.Exp, scale=-DEC)
                k.act(Ek, cum, AF.Exp, scale=DEC)
                k.act(Ea, cumx, AF.Exp, scale=-DEC)
                k.act(kk2, kf[hp], AF.Square, scale=kkv[:, hp:hp + 1])
                pb = nbig()
                k.mm(pb, bdm, kk2, start=True)
                k.act(nrm, pb, AF.Sqrt, bias=1e-24)
                k.recip(nrm, nrm)
                k.stt("dve", kkn, kf[hp], kkv[:, hp:hp + 1], nrm, ALU.mult, ALU.mult)
                for h in range(2):
                    arm = ARm[hp][h]
                    k.stt("dve", arm[:, 0:512], kkn, nhm[:, h:h + 1], Ea, ALU.mult, ALU.mult)
                    k.stt("dve", arm[:, 512:1024], rf[hp], hm[:, h:h + 1], Er[hp], ALU.mult, ALU.mult)
                k.tt("dve", t1, kkn, af_, ALU.mult)
                k.tt("dve", bT[hp], t1, Ek, ALU.mult)
                k.ts("pool", t2, af_, kav[:, hp:hp + 1], oka[:, hp:hp + 1], ALU.mult, ALU.add)
                k.tt("pool", kmod, kf[hp], t2, ALU.mult)
                k.tt("dve", kT[hp], kmod, Ek, ALU.mult)
                k.stt("dve", rkb[hp], rf[hp], rkv[:, hp:hp + 1], kmod, ALU.mult, ALU.mult)
            for c in range(4):
                cs = slice(c * 128, (c + 1) * 128)
                ot = otile[(sti * 4 + c) % 2]
                for hp in range(2):
                    arm = ARm[hp]
                    aT = [arm[h][:, 0:512][:, cs] for h in range(2)]
                    rT = [arm[h][:, 512:1024][:, cs] for h in range(2)]
                    arT = [arm[h].re("p (a t) -> p a t", a=2)[:, :, cs] for h in range(2)]
                    bTc = bT[hp][:, cs]
                    kTc = kT[hp][:, cs]
                    tpb = ntp()
                    k.tr(tpb[:, 0:128], bTc, ident)
                    k.tr(tpb[:, 128:256], kTc, ident)
                    k.tr(tpb[:, 256:384], vT[hp][:, cs], ident)
                    tmv = TMbk.re("p (a h c) -> p a h c", a=2, h=2)
                    for a_ in range(2):
                        for h in range(2):
                            k.cp("act" if h == 0 else "dve", tmv[:, a_, h, h * 64:(h + 1) * 64],
                                 tpb[:, a_ * 128 + h * 64: a_ * 128 + (h + 1) * 64])
                    k.cp("act", Vtk, tpb[:, 256:384])
                    sq4 = SQ.re("p (h a t) -> p h a t", h=2, a=2)
                    for h in range(2):
                        k.mm(sq4[:, h, 0, :], aT[h], bTc, start=(h == 0))
                        k.mm(sq4[:, h, 1, :], bTc, aT[h], start=False)
                    k.tt("dve", XY[0], SQ, mXY, ALU.mult)
                    bk4 = BK1.re("p (h a t) -> p h a t", h=2, a=2)
                    for h in range(2):
                        k.mm(bk4[:, h, :, :], kTc, arT[h], start=(h == 0))
                    k.tt("dve", XK, BK1, mBK, ALU.mult)
                    for h in range(2):
                        k.mm(pMrb[:, h * 128:(h + 1) * 128], bTc, rT[h], start=(h == 0))
                    k.tt("dve", XB, pMrb, mRB, ALU.mult)
                    xk4 = XK.re("p (h a t) -> p h a t", h=2, a=2)
                    for h in range(2):
                        k.mm(pW[:, h * 64:(h + 1) * 64], aT[h], sel, start=False)
                        k.mm(pW[:, 128 + h * 64: 128 + (h + 1) * 64], xk4[:, h, 0, :], Vtk[:, h * 64:(h + 1) * 64],
                             start=False)
                    k.cp("act", Wb[0], pW)
                    wcur = 0
                    for lv in range(7):
                        xy4 = XY[lv].re("p (h a t) -> p h a t", h=2, a=2)
                        for h in range(2):
                            wv = Wb[wcur].re("p (a h c) -> p a h c", a=2, h=2)[:, :, h, :]
                            pv = pW.re("p (a h c) -> p a h c", a=2, h=2)[:, :, h, :]
                            k.mm(pv, xy4[:, h, 1, :], wv, start=False)
                        k.cp("act" if lv % 2 == 0 else "dve", Wb[1 - wcur], pW)
                        wcur = 1 - wcur
                        if lv < 6:
                            last = (lv == 5)
                            first = True
                            for h in range(2):
                                k.mm(sq4[:, h, 1, :], xy4[:, h, 0, :], xy4[:, h, 1, :], start=first)
                                first = False
                                if not last:
                                    k.mm(sq4[:, h, 0, :], xy4[:, h, 1, :], xy4[:, h, 0, :], start=False)
                            k.cp("dve" if lv % 2 == 0 else "act", XY[lv + 1], SQ)
                    Wf = Wb[wcur]
                    xb4 = XB.re("p (h t) -> p h t", h=2)
                    for h in range(2):
                        k.mm(pR[:, h * 128:(h + 1) * 128], Wf[:, 0:128], xb4[:, h, :], start=(h == 0))
                    for h in range(2):
                        k.mm(pG[:, h * 64:(h + 1) * 64], Wf[:, 0:128], tmv[:, 0, h, h * 64:(h + 1) * 64], start=False)
                    for h in range(2):
                        k.mm(pH, tmv[:, 0, h, :], Wf[:, 128 + h * 64: 128 + (h + 1) * 64], start=False)
                        k.mm(pH, tmv[:, 1, h, :], Vtk[:, h * 64:(h + 1) * 64], start=False)
                    for h in range(2):
                        k.mm(pY2[:, h * 64:(h + 1) * 64], xb4[:, h, :], Wf[:, 128 + h * 64: 128 + (h + 1) * 64],
                             start=(h == 0))
                        k.mm(pY2[:, h * 64:(h + 1) * 64], xk4[:, h, 1, :], Vtk[:, h * 64:(h + 1) * 64], start=False)
                    k.mm(pBS, rkb[hp][:, cs], hsel, start=False)
                    k.cp("dve", bsf[hp], pBS)
                    for h in range(2):
                        k.stt("dve", RTm[:, h * 128:(h + 1) * 128], pR[:, h * 128:(h + 1) * 128], hm[:, h:h + 1],
                              rT[h], ALU.mult, ALU.add)
                    k.tt("dve", GT, pG, bdm, ALU.mult)
                    for h in range(2):
                        k.mm(pY2[:, h * 64:(h + 1) * 64], RTm[:, h * 128:(h + 1) * 128], Hb[hp], start=False)
                    k.mm(pH, GT, Hb[hp], start=False)
                    pc = Er[hp][:, c * 128 + 127: c * 128 + 128]
                    k.tt("dve", Zt, pH, Hf[hp], ALU.add)
                    k.act(Hf[hp], Zt, AF.Identity, scale=pc)
                    k.ts("dve", Hb[hp], Zt, pc, None, ALU.mult)
                    k.cp("act", ycp, pY2)
                    s1, s2, mean, m2, var, rstd = (st8[:, 0:2], st8[:, 2:4], st8[:, 4:6], st8[:, 6:8],
                                                   st8[:, 8:10], st8[:, 10:12])
                    k.reduce(s1, ycp.re("p (h c) -> p h c", h=2))
                    for h in range(2):
                        k.act(yn[:, h * 64:(h + 1) * 64], ycp[:, h * 64:(h + 1) * 64], AF.Square, accum=s2[:, h:h + 1])
                    k.ts("dve", mean, s1, 1.0 / 64, None, ALU.mult)
                    k.tt("dve", m2, mean, mean, ALU.mult)
                    k.stt("dve", var, s2, 1.0 / 64, m2, ALU.mult, ALU.subtract)
                    k.act(var, var, AF.Sqrt, bias=64e-5)
                    k.recip(rstd, var)
                    for h in range(2):
                        k.ts("dve", yn[:, h * 64:(h + 1) * 64], ycp[:, h * 64:(h + 1) * 64], mean[:, h:h + 1],
                             rstd[:, h:h + 1], ALU.subtract, ALU.mult)
                    k.tt("pool", yn, yn, lnx[:, hp * 128:(hp + 1) * 128], ALU.mult)
                    k.tt("pool", yn, yn, lnx[:, 256 + hp * 128: 256 + (hp + 1) * 128], ALU.add)
                    for h in range(2):
                        k.stt("dve", yn[:, h * 64:(h + 1) * 64], Vtk[:, h * 64:(h + 1) * 64], bsf[hp][:, h:h + 1],
                              yn[:, h * 64:(h + 1) * 64], ALU.mult, ALU.add)
                    k.tt("dve", ot[:, hp * 128:(hp + 1) * 128], yn,
                         gtok[:, c * 256 + hp * 128: c * 256 + (hp + 1) * 128], ALU.mult)
                tg = t0 + c * 128
                qi, tl_ = tg // TQ, tg % TQ
                k.dma("sp", srcq[qi].ap()[tl_:tl_ + 128, :], ot, f"oq{(sti * 4 + c) % 2}")
                if sti + 1 < n_super:
                    s1_tile(sti + 1, c)
            if (t0 + 512) % TQ == 0:
                qi = t0 // TQ
                stores = [o for o in S.all_ops if o.slot in ("oq0", "oq1")]
                s_ap, d_ap = srcq[qi].ap().opt(), dstq[qi].ap().opt()
                cco = S.op("pool", lambda e, s_ap=s_ap, d_ap=d_ap: e.collective_compute(
                    "AllGather", ALU.bypass, replica_groups=[[0, 1, 2, 3], [4, 5, 6, 7]], ins=[s_ap], outs=[d_ap]),
                    extra_deps=stores[-2:], slot=f"cc{qi}", inc=1)
                cc_ops.append(cco)

        if debug:
            S.barrier()
            dt_ = AFa.alloc(256, "dbgt")
            db_ = ABa.alloc(256, "dbgb")
            for qi in range((n_super * 512 + TQ - 1) // TQ):
                for tt in range(16):
                    if qi * TQ + tt * 128 >= n_super * 512:
                        break
                    k.dma("sp", db_, srcq[qi].ap()[tt * 128:(tt + 1) * 128, :], "dbg1")
                    k.cp("dve", dt_, db_)
                    k.dma("sp", dbg["rw"][qi * TQ + tt * 128: qi * TQ + (tt + 1) * 128, :], dt_, "dbg2")


        if do_phase2:
            S.barrier()
            AFa.reset(f_mark0)
            ABa.reset(b_mark0)
            banks = [BIG[0], BIG[1], SQ, BK1, M1b, M2b]
            bi = [0]

            def nb():
                b = banks[bi[0] % 6]
                bi[0] += 1
                return b

            g1bc = AFa.alloc(2048, "g1bc")
            g2bc = AFa.alloc(2048, "g2bc")
            dg = AFa.alloc(128, "dg")
            make_gbc(g1bc, g2bc, dg)
            cpk = AFa.alloc(272, "cpk")
            k.dma("sp", cpk, conv_pk, "c_cp")
            cpk3 = cpk.re("p (g j) -> p g j", g=8)
            xt2 = [AFa.alloc(2048, f"xq{i}") for i in range(4)]
            ub = [AFa.alloc(544, f"ub{i}") for i in range(8)]
            sm2 = [AFa.alloc(4, f"sm2{i}") for i in range(2)]
            hT2 = ABa.alloc(8192, "hT2")
            wst = ABa.alloc(12288, "wst")
            wslab = [Tl(wst.ap[:, 0:4096], Res("ws0")), Tl(wst.ap[:, 4096:8192], Res("ws1"))]
            wsi = [0]
            for sti in range(-1, n_st2):
                halo = sti < 0
                ntt = 1 if halo else 4
                W = 128 * ntt
                row0 = 0 if halo else 128 + sti * 512
                fA, bA = AFa.mark(), ABa.mark()
                xn2 = [ABa.alloc(2048, f"xn2{i}") for i in range(ntt)]
                for tt in range(ntt):
                    k.dma("sp", xt2[tt], xq[row0 + tt * 128: row0 + (tt + 1) * 128, :], f"xq{tt}")
                    norm_tile(xt2[tt], xn2[tt], gm1, sh1, hT2, tt, xn2[tt], sm2[tt % 2])
                transposes(xn2, gm1, sh1, hT2, ntt)
                accb = AFa.alloc(512, "accb")
                acc2 = AFa.alloc(512, "acc2")
                y2 = AFa.alloc(512, "y2")
                sgt = AFa.alloc(512, "sgt")
                tmpa = [AFa.alloc(512, f"tmpa{i}") for i in range(2)]
                if not halo:
                    ocT = ABa.alloc(8192, "ocT")
                    gst = [ABa.alloc(1024, f"gst{i}") for i in range(4)]
                    orw = [ABa.alloc(1024, f"orw{i}") for i in range(4)]
                    gacc = AFa.alloc(1024, "gacc")
                for cg in range(8):
                    slab = wslab[wsi[0] % 2]
                    k.dma("pool", slab, w_conv_t[cg * 128:(cg + 1) * 128, :], f"ws{wsi[0] % 2}")
                    wsi[0] += 1
                    pv, pg = nb(), nb()
                    for kc in range(16):
                        k.mm(pv[:, 0:W], slab[:, kc * 256: kc * 256 + 128], hT2[:, kc * 512: kc * 512 + W],
                             start=(kc == 0), stop=(kc == 15))
                    for kc in range(16):
                        k.mm(pg[:, 0:W], slab[:, kc * 256 + 128: kc * 256 + 256], hT2[:, kc * 512: kc * 512 + W],
                             start=(kc == 0), stop=(kc == 15))
                    k.act(sgt[:, 0:W], pg[:, 0:W], AF.Sigmoid)
                    u = ub[cg]
                    if halo:
                        k.tt("dve", y2[:, 0:128], pv[:, 0:128], sgt[:, 0:128], ALU.mult)
                        k.ts("dve", u[:, 512:542], y2[:, 98:128], hflag, None, ALU.mult)
                        continue
                    k.cp("pool", u[:, 0:30], u[:, 512:542])
                    k.tt("dve", u[:, 30:542], pv, sgt, ALU.mult)
                    wk = cpk3[:, cg, :]
                    k.ts("dve", accb, u[:, 0:512], wk[:, 0:1], wk[:, 31:32], ALU.mult, ALU.add)
                    NDV = 22
                    for kk in range(1, NDV):
                        k.stt("dve", accb, u[:, kk:kk + 512], wk[:, kk:kk + 1], accb, ALU.mult, ALU.add)
                    k.act(acc2, u[:, NDV:NDV + 512], AF.Identity, scale=wk[:, NDV:NDV + 1])
                    for kk in range(NDV + 1, 31):
                        tb = tmpa[kk % 2]
                        k.act(tb, u[:, kk:kk + 512], AF.Identity, scale=wk[:, kk:kk + 1])
                        k.tt("pool", acc2, acc2, tb, ALU.add)
                    k.tt("pool", accb, accb, acc2, ALU.add)
                    k.act(y2, accb, AF.Square)
                    pmn, pmq = nb(), nb()
                    k.mm(pmn, bo64, accb, start=True)
                    k.mm(pmq, bo64, y2, start=True)
                    k.act(sgt, pmn, AF.Square)
                    k.tt("dve", y2, pmq, sgt, ALU.subtract)
                    k.act(y2, y2, AF.Sqrt, bias=1e-5)
                    k.recip(y2, y2)
                    k.tt("dve", accb, accb, pmn, ALU.subtract)
                    k.tt("pool", accb, accb, y2, ALU.mult)
                    k.act(ocT[:, cg * 512:(cg + 1) * 512], accb, AF.Silu, bias=wk[:, 33:34], scale=wk[:, 32:33])
                if halo:
                    S.barrier()
                    AFa.reset(fA)
                    ABa.reset(bA)
                    if stop_after == "halo":
                        break
                    continue
                if stop_after == "conv":
                    break
                for tt in range(4):
                    r0 = sti * 512 + tt * 128
                    for qd in range(4):
                        src = dstq[qd].ap().rearrange("(r t) c -> t r c", r=4)[r0:r0 + 128, :, :]
                        k.dma("sp", gst[qd].re("p (r c) -> p r c", r=4), src, f"gq{qd}", extra_deps=cc_ops)
                    k.ts("dve", gacc, gst[0], qfl[:, 0:1], None, ALU.mult)
                    for qd in range(1, 4):
                        k.stt("dve", gacc if qd < 3 else orw[tt], gst[qd], qfl[:, qd:qd + 1], gacc, ALU.mult, ALU.add)
                for kc in range(8):
                    tpb = ntp()
                    for tt in range(4):
                        k.tr(tpb[:, tt * 128:(tt + 1) * 128], orw[tt][:, kc * 128:(kc + 1) * 128], ident)
                    k.cp("act" if kc % 2 == 0 else "dve", ocT[:, (8 + kc) * 512:(9 + kc) * 512], tpb[:, 0:512])
                if stop_after == "gather":
                    break
                if debug and sti == 0:
                    dbt = AFa.alloc(512, "dbt")
                    for kc in range(16):
                        k.cp("dve", dbt, ocT[:, kc * 512:(kc + 1) * 512])
                        k.dma("sp", dbg["oc"][:, kc * 512:(kc + 1) * 512], dbt, "dbg3")
                for dh in range(8):
                    slab = wslab[wsi[0] % 2]
                    k.dma("pool", slab, w_out_t[dh * 128:(dh + 1) * 128, :], f"ws{wsi[0] % 2}")
                    wsi[0] += 1
                    for tp2 in range(2):
                        pb = nb()
                        for j in range(2):
                            tt = tp2 * 2 + j
                            for kc in range(16):
                                k.mm(pb[:, j * 256:(j + 1) * 256], ocT[:, kc * 512 + tt * 128: kc * 512 + (tt + 1) * 128],
                                     slab[:, kc * 256:(kc + 1) * 256], start=(kc == 0), stop=(kc == 15))
                        for j in range(2):
                            tt = tp2 * 2 + j
                            xs = xt2[tt][:, dh * 256:(dh + 1) * 256]
                            k.tt("dve", tmpa[j][:, 0:256], pb[:, j * 256:(j + 1) * 256], g1bc[:, dh * 256:(dh + 1) * 256],
                                 ALU.mult)
                            k.tt("pool", xs, xs, tmpa[j][:, 0:256], ALU.add)
                if debug:
                    for tt in range(4):
                        k.dma("sp", dbg["x1"][sti * 512 + tt * 128: sti * 512 + (tt + 1) * 128, :], xt2[tt], "dbg4")
                S.barrier()
                AFa.reset(fA)
                ABa.reset(bA)
                if stop_after == "wout":
                    break
                xn2 = [ABa.alloc(2048, f"xn2b{i}") for i in range(4)]
                gT = ABa.alloc(NFF * 512, "gT")
                nfbc = AFa.alloc(2048, "nfbc")
                k.dma("sp", nfbc, nf_bc_d, "c_nf")
                sat = [AFa.alloc(512, f"sat{i}") for i in range(2)]
                mix2 = [AFa.alloc(512, f"mix2{i}") for i in range(2)]
                w1b = [Tl(wst.ap[:, i * 2048:(i + 1) * 2048], Res(f"w1b{i}")) for i in range(2)]
                w3b = [Tl(wst.ap[:, 4096 + i * 2048: 4096 + (i + 1) * 2048], Res(f"w3b{i}")) for i in range(2)]
                w2b = [Tl(wst.ap[:, 8192 + i * 2048: 8192 + (i + 1) * 2048], Res(f"w2b{i}")) for i in range(2)]
                for tt in range(4):
                    norm_tile(xt2[tt], xn2[tt], gm2, sh2, hT2, tt, xn2[tt], sm2[tt % 2])
                transposes(xn2, gm2, sh2, hT2, 4)
                for f in range(NFF):
                    w1t, w3t = w1b[f % 2], w3b[f % 2]
                    k.dma("pool", w1t, w_ff1_t[f * 128:(f + 1) * 128, :], f"w1_{f % 2}")
                    k.dma("pool", w3t, w_ff3_t[f * 128:(f + 1) * 128, :], f"w3_{f % 2}")
                    pa, pb3 = nb(), nb()
                    for kc in range(16):
                        k.mm(pa, w1t[:, kc * 128:(kc + 1) * 128], hT2[:, kc * 512:(kc + 1) * 512],
                             start=(kc == 0), stop=(kc == 15))
                    for kc in range(16):
                        k.mm(pb3, w3t[:, kc * 128:(kc + 1) * 128], hT2[:, kc * 512:(kc + 1) * 512],
                             start=(kc == 0), stop=(kc == 15))
                    sa = sat[f % 2]
                    k.act(sa, pa, AF.Silu)
                    k.tt("dve", gT[:, f * 512:(f + 1) * 512], pb3, sa, ALU.mult)
                if stop_after == "ffn_up":
                    break
                for dgi in range(4):
                    accs = [banks[(dgi % 2) * 0 + i] for i in range(4)]
                    for fb in range(NFF // 4):
                        w2t = w2b[fb % 2]
                        src = w_ff2[fb * 512:(fb + 1) * 512, dgi * 512:(dgi + 1) * 512].rearrange("(f p) c -> p f c", p=128)
                        k.dma("pool", w2t.re("p (f c) -> p f c", f=4), src, f"w2_{fb % 2}")
                        for fi in range(4):
                            f = fb * 4 + fi
                            for tt in range(4):
                                k.mm(accs[tt], gT[:, f * 512 + tt * 128: f * 512 + (tt + 1) * 128],
                                     w2t[:, fi * 512:(fi + 1) * 512], start=(f == 0), stop=(f == NFF - 1))
                    for tt in range(4):
                        xs = xt2[tt][:, dgi * 512:(dgi + 1) * 512]
                        k.tt("dve", mix2[tt % 2], accs[tt], g2bc[:, dgi * 512:(dgi + 1) * 512], ALU.mult)
                        k.tt("pool", xs, xs, mix2[tt % 2], ALU.add)
                for tt in range(4):
                    xt = xt2[tt]
                    sm = sm2[tt % 2]
                    ssq, std, rstd = sm[:, 0:1], sm[:, 1:2], sm[:, 2:3]
                    k.act(xn2[tt], xt, AF.Square, accum=ssq)
                    k.act(std, ssq, AF.Sqrt, bias=1e-6, scale=1.0 / D)
                    k.recip(rstd, std)
                    k.stt("dve", xt, xt, rstd, nfbc, ALU.mult, ALU.mult)
                    k.dma("sp", out_d[sti * 512 + tt * 128: sti * 512 + (tt + 1) * 128, :], xt, f"out{tt}")
                S.barrier()
                AFa.reset(fA)
                ABa.reset(bA)

        S.barrier()
        nsig = S.emit()
        print("signals", nsig, "arena peaks f32", AFa.peak, "bf16", ABa.peak,
              "ops", {e: len(S.ops[e]) for e in ENGS})
    return nc


def host_inputs(inp, core, debug=False):
    b, q = core // 4, core % 4
    f = np.float32
    x = inp["x"]
    d = {}
    d["xb"] = np.ascontiguousarray(x[b])
    xq = np.zeros((TQ + 128, D), f)
    if q > 0:
        xq[:128] = x[b, q * TQ - 128: q * TQ]
    xq[128:] = x[b, q * TQ:(q + 1) * TQ]
    d["xq"] = xq

    def fm(v, n):
        return np.ascontiguousarray(np.asarray(v, f).reshape(n, 128).T)

    d["cT"] = fm(inp["c"][b], 16)
    d["w_ada"] = np.ascontiguousarray(inp["w_ada"][0])
    d["b_adaT"] = fm(inp["b_ada"][0], 96)
    d["n1T"] = fm(inp["norm1_g"][0], 16)
    d["n2T"] = fm(inp["norm2_g"][0], 16)
    d["nf_bc"] = np.ascontiguousarray(np.broadcast_to(inp["norm_f_g"][None, :], (128, D))).astype(f)
    w_in = inp["w_in"][0]
    CC = 2048
    ch = slice(q * 256, (q + 1) * 256)
    cols = np.concatenate([
        CC + np.arange(q * 256, (q + 1) * 256),
        CC + 1024 + np.arange(q * 256, (q + 1) * 256),
        CC + 2048 + np.arange(q * 256, (q + 1) * 256),
        CC + 3072 + 192 + np.arange(256),
        CC + 3072 + np.arange(96),
        CC + 3072 + 96 + np.arange(96),
    ])
    d["w_rw"] = np.ascontiguousarray(w_in[:, cols])
    mu_all = inp["tshift_mu"][0][cols - CC]
    vecs = np.zeros((128, 64), f)
    for g in range(8):
        vecs[:, g] = mu_all[g * 128:(g + 1) * 128]
    vecs[:96, 8] = mu_all[1024:1120]
    vecs[:96, 9] = mu_all[1120:1216]
    for j, nm in enumerate(["w0", "a0", "k_k", "k_a"]):
        v = inp[nm][0][ch]
        vecs[:, 10 + 2 * j] = v[0:128]
        vecs[:, 11 + 2 * j] = v[128:256]
    rk = inp["r_k"][0].reshape(-1)[ch]
    vecs[:, 18] = rk[0:128]
    vecs[:, 19] = rk[128:256]
    vecs[:, 20:36] = fm(inp["c"][b], 16)
    d["vecs"] = vecs
    d["w_w2c"] = np.ascontiguousarray(inp["w_w2"][0][:, ch])
    d["a_w2c"] = np.ascontiguousarray(inp["a_w2"][0][:, ch])
    d["g_w2c"] = np.ascontiguousarray(inp["g_w2"][0][:, ch])
    lnx = np.concatenate([inp["lnx_g"][0][ch], inp["lnx_b"][0][ch]])
    d["lnx_bc"] = np.ascontiguousarray(np.broadcast_to(lnx[None, :], (128, 512))).astype(f)
    wc = w_in[:, :2048].reshape(16, 128, 2, 8, 128)
    d["w_conv_t"] = np.ascontiguousarray(wc.transpose(3, 1, 0, 2, 4)).reshape(8 * 128, 16 * 256)
    cp = np.zeros((128, 8, 34), f)
    cw = inp["conv_w"][0]
    for cg in range(8):
        cs = slice(cg * 128, (cg + 1) * 128)
        cp[:, cg, 0:31] = cw[:, cs].T
        cp[:, cg, 31] = inp["conv_b"][0][cs]
        cp[:, cg, 32] = inp["conv_gn_g"][0][cs]
        cp[:, cg, 33] = inp["conv_gn_b"][0][cs]
    d["conv_pk"] = cp.reshape(128, 8 * 34)
    wo = inp["w_out"][0].reshape(16, 128, 8, 256)
    d["w_out_t"] = np.ascontiguousarray(wo.transpose(2, 1, 0, 3)).reshape(8 * 128, 16 * 256)
    for nm in ("w_ff1", "w_ff3"):
        w = inp[nm][0].reshape(16, 128, NFF, 128)
        d[nm + "_t"] = np.ascontiguousarray(w.transpose(2, 1, 0, 3)).reshape(NFF * 128, 16 * 128)
    d["w_ff2"] = np.ascontiguousarray(inp["w_ff2"][0])
    cb = np.zeros((128, 2560), f)
    cb[:, 0:128] = np.eye(128)
    p = np.arange(128)[:, None]
    c = np.arange(128)[None, :]
    low = (c < p).astype(f)
    upS = (p < c).astype(f)
    upI = (p <= c).astype(f)
    m = np.zeros((128, 2, 2, 128), f)
    m[:, :, 0, :] = low[:, None, :]
    m[:, :, 1, :] = upS[:, None, :]
    cb[:, 128:640] = m.reshape(128, 512)
    m = np.zeros((128, 2, 2, 128), f)
    m[:, :, 0, :] = upS[:, None, :]
    m[:, :, 1, :] = upI[:, None, :]
    cb[:, 640:1152] = m.reshape(128, 512)
    m = np.zeros((128, 2, 128), f)
    m[:, :, :] = upI[:, None, :]
    cb[:, 1152:1408] = m.reshape(128, 256)
    bd = ((p // 64) == (c // 64)).astype(f)
    cb[:, 1408:1536] = bd
    s = np.zeros((128, 64), f)
    s[np.arange(128), np.arange(128) % 64] = 1
    cb[:, 1536:1600] = s
    cb[:64, 1600] = 1
    cb[64:, 1601] = 1
    d["cst_bf"] = cb
    cf = np.zeros((128, 1024), f)
    sm = np.ones((128, 512), f)
    sm[:, ::128] = 0
    cf[:, 0:512] = sm
    cf[:, 512:640] = np.eye(128)
    cf[:, 640:768] = 1.0
    cf[:, 768:896] = bd / 64.0
    cf[:64, 896] = 1
    cf[64:, 897] = 1
    cf[:64, 898] = -1
    cf[64:, 899] = -1
    cf[:, 900] = 1.0 if q > 0 else 0.0
    cf[:, 904 + q] = 1.0
    d["cst_f"] = cf
    return d


_NC_CACHE = {}


def kernel(**inputs):
    inp = {k_: np.asarray(v) for k_, v in inputs.items()}
    if "nc" not in _NC_CACHE:
        _NC_CACHE["nc"] = build_program()
    nc = _NC_CACHE["nc"]
    in_maps = [host_inputs(inp, i) for i in range(8)]
    res = run_bass_kernel_spmd(nc, in_maps, core_ids=list(range(8)))
    out = np.zeros((2, T, D), np.float32)
    for i in range(8):
        b, q = i // 4, i % 4
        out[b, q * TQ:(q + 1) * TQ] = res.results[i]["out"]
    return out
```
